# Optimizing a Trainium2 kernel written in Bass

```python
import math
import jax, jax.numpy as jnp
from jax import lax
import numpy as np

D_MODEL = 2048
BATCH = 2
SEQ = 8192
DEPTH = 2

MEM_LEN = 256
D_CONV = 1024
CONV_WIDTH = 31
CONV_PAD = (CONV_WIDTH - 1) // 2
D_FOURIER = 512
N_FOURIER_GROUPS = 4
FOURIER_GROUP = D_FOURIER // N_FOURIER_GROUPS
N_MEM_HEADS = 4
MEM_HEAD_DIM = 128
D_MEM_Q = N_MEM_HEADS * MEM_HEAD_DIM
N_BRANCHES = 3
D_FF = 5632
D_IN = 2 * D_CONV + D_FOURIER + D_MEM_Q + N_BRANCHES * D_MODEL
DN_ALPHA = (2.0 * DEPTH) ** 0.25
DN_BETA = (8.0 * DEPTH) ** -0.25
LN_EPS = 1e-5

kernel_name = "hybrid_conv_fourier_memattn_deepnorm_encoder"


def layer_norm(x, g, b):
    xf = x.astype(jnp.float32)
    mu = jnp.mean(xf, axis=-1, keepdims=True)
    xc = xf - mu
    var = jnp.mean(xc * xc, axis=-1, keepdims=True)
    y = xc * lax.rsqrt(var + LN_EPS) * g.astype(jnp.float32) + b.astype(jnp.float32)
    return y.astype(x.dtype)


def swiglu_ffn(x, w13, w2):
    h = x @ w13
    a, g = jnp.split(h, 2, axis=-1)
    return (jax.nn.silu(g) * a) @ w2


def conformer_conv_branch(u_val, u_gate, conv_w, conv_b, ln_g, ln_b, w_proj):
    h = u_val * jax.nn.sigmoid(u_gate)
    h = lax.conv_general_dilated(
        h, conv_w[:, None, :].astype(h.dtype),
        window_strides=(1,), padding=[(CONV_PAD, CONV_PAD)],
        dimension_numbers=("NWC", "WIO", "NWC"),
        feature_group_count=D_CONV) + conv_b
    h = jax.nn.silu(layer_norm(h, ln_g, ln_b))
    return h @ w_proj


def fourier_branch(u, w_lin):
    b, s, _ = u.shape
    ug = u.reshape(b, s, N_FOURIER_GROUPS, FOURIER_GROUP).astype(jnp.float32)
    f = jnp.fft.fft2(ug, axes=(1, 3), norm="ortho").real
    f = f.reshape(b, s, D_FOURIER).astype(u.dtype)
    return f @ w_lin


def memory_cross_attention(q_flat, mem, w_kv, w_o):
    b, s, _ = q_flat.shape
    q = q_flat.reshape(b, s, N_MEM_HEADS, MEM_HEAD_DIM)
    kv = mem @ w_kv
    k, v = jnp.split(kv, 2, axis=-1)
    k = k.reshape(b, MEM_LEN, N_MEM_HEADS, MEM_HEAD_DIM)
    v = v.reshape(b, MEM_LEN, N_MEM_HEADS, MEM_HEAD_DIM)
    scores = jnp.einsum("bshd,bmhd->bhsm", q, k).astype(jnp.float32) * (MEM_HEAD_DIM ** -0.5)
    p = jax.nn.softmax(scores, axis=-1).astype(v.dtype)
    o = jnp.einsum("bhsm,bmhd->bshd", p, v).reshape(b, s, D_MEM_Q)
    return o @ w_o


def setup_inputs(seed: int = 0) -> dict:
    key = jax.random.key(seed)
    ks = jax.random.split(key, 24)
    f32 = jnp.float32

    def nrm(k, shape, scale):
        return jax.random.normal(k, shape, f32) * scale

    def gain(k, shape):
        return 1.0 + 0.01 * jax.random.normal(k, shape, f32)

    def bias(k, shape):
        return 0.01 * jax.random.normal(k, shape, f32)

    L, D = DEPTH, D_MODEL
    return {
        "x": nrm(ks[0], (BATCH, SEQ, D), 1.0),
        "mem": nrm(ks[1], (BATCH, MEM_LEN, D), 1.0),
        "ffn1_w13": nrm(ks[2], (L, D, 2 * D_FF), D ** -0.5),
        "ffn1_w2": nrm(ks[3], (L, D_FF, D), DN_BETA * D_FF ** -0.5),
        "ln1_g": gain(ks[4], (L, D)),
        "ln1_b": bias(ks[5], (L, D)),
        "w_in": nrm(ks[6], (L, D, D_IN), D ** -0.5),
        "b_in": bias(ks[7], (L, D_IN)),
        "conv_w": nrm(ks[8], (L, CONV_WIDTH, D_CONV), CONV_WIDTH ** -0.5),
        "conv_b": bias(ks[9], (L, D_CONV)),
        "conv_ln_g": gain(ks[10], (L, D_CONV)),
        "conv_ln_b": bias(ks[11], (L, D_CONV)),
        "w_conv_out": nrm(ks[12], (L, D_CONV, D), D_CONV ** -0.5),
        "w_fourier": nrm(ks[13], (L, D_FOURIER, D), D_FOURIER ** -0.5),
        "w_mem_kv": nrm(ks[14], (L, D, 2 * D_MEM_Q), D ** -0.5),
        "w_mem_o": nrm(ks[15], (L, D_MEM_Q, D), D_MEM_Q ** -0.5),
        "w_out": nrm(ks[16], (L, D, D), DN_BETA * D ** -0.5),
        "ln2_g": gain(ks[17], (L, D)),
        "ln2_b": bias(ks[18], (L, D)),
        "ffn2_w13": nrm(ks[19], (L, D, 2 * D_FF), D ** -0.5),
        "ffn2_w2": nrm(ks[20], (L, D_FF, D), DN_BETA * D_FF ** -0.5),
        "ln3_g": gain(ks[21], (L, D)),
        "ln3_b": bias(ks[22], (L, D)),
    }


def reference(x, mem, ffn1_w13, ffn1_w2, ln1_g, ln1_b, w_in, b_in,
              conv_w, conv_b, conv_ln_g, conv_ln_b, w_conv_out, w_fourier,
              w_mem_kv, w_mem_o, w_out, ln2_g, ln2_b,
              ffn2_w13, ffn2_w2, ln3_g, ln3_b):
    c0 = 0
    c1 = c0 + D_CONV
    c2 = c1 + D_CONV
    c3 = c2 + D_FOURIER
    c4 = c3 + D_MEM_Q
    b, s, _ = x.shape
    for l in range(DEPTH):
        x = layer_norm(DN_ALPHA * x + 0.5 * swiglu_ffn(x, ffn1_w13[l], ffn1_w2[l]),
                       ln1_g[l], ln1_b[l])

        h = x @ w_in[l] + b_in[l]
        y_conv = conformer_conv_branch(h[..., c0:c1], h[..., c1:c2], conv_w[l], conv_b[l],
                                       conv_ln_g[l], conv_ln_b[l], w_conv_out[l])
        y_four = fourier_branch(h[..., c2:c3], w_fourier[l])
        y_mem = memory_cross_attention(h[..., c3:c4], mem, w_mem_kv[l], w_mem_o[l])
        gates = jax.nn.sigmoid(h[..., c4:]).reshape(b, s, N_BRANCHES, D_MODEL)
        merged = (gates[:, :, 0] * y_conv + gates[:, :, 1] * y_four
                  + gates[:, :, 2] * y_mem)
        x = layer_norm(DN_ALPHA * x + merged @ w_out[l], ln2_g[l], ln2_b[l])

        x = layer_norm(DN_ALPHA * x + 0.5 * swiglu_ffn(x, ffn2_w13[l], ffn2_w2[l]),
                       ln3_g[l], ln3_b[l])
    return x
```

```python
import numpy as np
import ml_dtypes
import concourse.bass as bass
import concourse.mybir as mybir
from concourse.bass_utils import run_bass_kernel_spmd

F32 = mybir.dt.float32
BF16 = mybir.dt.bfloat16
AF = mybir.ActivationFunctionType
ALU = mybir.AluOpType
AX = mybir.AxisListType

NCORES = 8
D = 2048
SEQ = 8192
TOK = 2048
TT = 512
NT = TOK // TT
DFF = 5632
NJ = DFF // 128
NDC = D // 128
DEPTH = 2
ALPHA = (2.0 * DEPTH) ** 0.25
LN_EPS = 1e-5
HALO = 15
SCALE = 128.0 ** -0.5

V_LN1G, V_LN1B, V_LN2G, V_LN2B, V_LN3G, V_LN3B = 0, 16, 32, 48, 64, 80
V_BIN = 96
V_CONVB = 168
V_CLNG = 176
V_CLNB = 184
V_CONVW = 192
V_N = 440


class Buf:
    __slots__ = ("w", "r", "dsem", "dcount", "key")
    _n = 0

    def __init__(self):
        self.w = None
        self.r = {}
        self.dsem = None
        self.dcount = 0
        Buf._n += 1
        self.key = "b%d" % Buf._n


class Eng:
    def __init__(self, nc, name, h, is_pe=False):
        self.h = h
        self.key = "e_" + name
        self.sem = nc.alloc_semaphore("sem_" + name)
        self.count = 0
        self.waited = {}
        self.is_pe = is_pe


class Ctx:
    def __init__(self, nc):
        self.nc = nc
        self.pe = Eng(nc, "pe", nc.tensor, True)
        self.act = Eng(nc, "act", nc.scalar)
        self.dve = Eng(nc, "dve", nc.vector)
        self.pool = Eng(nc, "pool", nc.gpsimd)
        self.sp = Eng(nc, "sp", nc.sync)
        self.dram_out_bufs = []

    def _deps(self, eng, reads, writes):
        deps = []
        for b in reads:
            if b.w is not None:
                deps.append(b.w)
        for b in writes:
            if b.w is not None:
                deps.append(b.w)
            deps.extend(b.r.values())
        for (key, sem, val) in deps:
            if key == eng.key and eng.is_pe:
                continue
            if eng.waited.get(key, 0) >= val:
                continue
            eng.waited[key] = val
            eng.h.wait_ge(sem, val)

    def op(self, eng, fn, reads=(), writes=(), signal=True):
        self._deps(eng, reads, writes)
        ins = fn(eng.h)
        if signal:
            eng.count += 1
            ins.then_inc(eng.sem, 1)
            tok = (eng.key, eng.sem, eng.count)
        else:
            tok = (eng.key, eng.sem, eng.count + 1)
        for b in reads:
            b.r[tok[0]] = tok
        for b in writes:
            b.w = tok
            b.r = {}

    def dma(self, q, out_ap, in_ap, reads, writes):
        self._deps(q, reads, writes)
        dst = writes[0]
        if dst.dsem is None:
            dst.dsem = self.nc.alloc_semaphore("d_" + dst.key)
        dst.dcount += 16
        tok = (dst.key, dst.dsem, dst.dcount)
        q.h.dma_start(out=out_ap, in_=in_ap).then_inc(dst.dsem, 16)
        for b in reads:
            b.r[tok[0]] = tok
        for b in writes:
            b.w = tok
            b.r = {}

    def mm_group(self, out_ap, bank, terms, signal=True, start=True, stop=True):
        n = len(terms)
        for i, (l, r, rb) in enumerate(terms):
            last = i == n - 1
            self.op(self.pe,
                    (lambda l=l, r=r, st=(start and i == 0), sp_=(stop and last):
                     lambda h: h.matmul(out_ap, l, r, start=st, stop=sp_))(),
                    reads=rb, writes=[bank], signal=(signal and last))

    def finish(self):
        for b in self.dram_out_bufs:
            if b.dsem is not None:
                self.sp.h.wait_ge(b.dsem, b.dcount)


class Tile:
    __slots__ = ("ap", "buf")

    def __init__(self, ap, buf=None):
        self.ap = ap
        self.buf = buf if buf is not None else Buf()


def build_program(has_post, has_pre, first, last, n2048, n5632, n1024, n512):
    nc = bass.Bass("TRN2", target_bir_lowering=False)
    C = Ctx(nc)
    pe, act, dve, pool, sp = C.pe, C.act, C.dve, C.pool, C.sp

    def din(name, shape, dt=F32):
        return nc.dram_tensor(name, list(shape), dt, kind="ExternalInput")

    def dout(name, shape, dt=F32):
        t = nc.dram_tensor(name, list(shape), dt, kind="ExternalOutput")
        b = Buf()
        C.dram_out_bufs.append(b)
        return t, b

    xin = din("xin", [D, TOK])
    w2048 = din("w2048", [n2048, 128, 16 * 128])
    w5632 = din("w5632", [n5632, 128, NJ * 128])
    vecs = din("vecs", [128, 2 * V_N])
    consts = din("consts", [128, 128 + 128 + 256])
    if has_post:
        w1024 = din("w1024", [n1024, 128, 8 * 128])
        w512 = din("w512", [n512, 128, 4 * 128])
        gluh = din("gluh", [1024, TOK + 2 * HALO])
        ufull = din("ufull", [SEQ, 1024], BF16)
        ftab = din("ftab", [NT, 32, 128, 2 * 2 * 512], BF16)
        memT = din("memT", [D, 256])
    if has_pre:
        x1o, x1o_b = dout("x1o", [D, TOK])
        gluo, gluo_b = dout("gluo", [1024, TOK])
        uo, uo_b = dout("uo", [TOK, 1024], BF16)
    if last:
        yo, yo_b = dout("yo", [D, TOK])

    arena_bytes = 212600
    arena = nc.alloc_sbuf_tensor("arena", [128, arena_bytes], mybir.dt.uint8)
    base = nc.lookup_mloc(arena).addr
    cur = [0]

    def sb(name, shape, dt, at=None):
        nbytes = int(np.prod(shape[1:])) * (2 if dt == BF16 else 4)
        if at is None:
            off = cur[0]
            cur[0] += (nbytes + 63) // 64 * 64
            assert cur[0] <= arena_bytes, (name, cur[0])
        else:
            off = at
        return nc.alloc_sbuf_tensor_at(name, list(shape), dt, offset=base + off), off

    X, _ = sb("X", [128, NDC, TT], F32)
    XB, _ = sb("XB", [128, NDC, TT], BF16)
    ACTR, act_off = sb("ACTR", [128, NJ, TT], BF16)
    Xb = [Buf() for _ in range(NDC)]
    XBb = [Buf() for _ in range(NDC)]
    ACTb = [Buf() for _ in range(NJ)]
    CONV, _ = sb("CONV", [128, 8, TT], F32, at=act_off)
    HCB, _ = sb("HCB", [128, 8, TT], BF16, at=act_off + 16 * 1024)
    MRG, _ = sb("MRG", [128, NDC, TT], BF16, at=act_off + 24 * 1024)

    def conv_bufs(cc):
        return [ACTb[2 * cc], ACTb[2 * cc + 1]]

    NS2048 = 3
    WA = [sb("WA%d" % i, [128, 2, 16, 128], BF16)[0] for i in range(NS2048)]
    WAb = [Buf() for _ in range(NS2048)]
    WB = [sb("WB%d" % i, [128, NJ, 128], BF16)[0] for i in range(2)]
    WBb = [Buf() for _ in range(2)]
    VEC, _ = sb("VEC", [128, 2 * V_N], F32)
    VECb = Buf()
    IDENT, _ = sb("IDENT", [128, 128], F32)
    ONES, _ = sb("ONES", [128, 128], F32)
    DFTC, _ = sb("DFTC", [128, 256], BF16)
    CONb = Buf()
    DFTCb = Buf()
    ACC1, _ = sb("ACC1", [128, TT], F32); ACC1b = Buf()
    ACC2, _ = sb("ACC2", [128, TT], F32); ACC2b = Buf()
    SQ = [sb("SQ%d" % i, [128, TT], F32)[0] for i in range(2)]
    SQb = [Buf(), Buf()]
    MEAN, _ = sb("MEAN", [128, TT], F32); MEANb = Buf()
    RSTD, _ = sb("RSTD", [128, TT], F32); RSTDb = Buf()
    TMP = [sb("TMP%d" % i, [128, TT], F32)[0] for i in range(3)]
    TMPb = [Buf(), Buf(), Buf()]
    GHo = [sb("GH%d" % i, [128, TT + 2 * HALO], F32) for i in range(2)]
    GH = [g[0] for g in GHo]
    GHb = [Buf(), Buf()]
    if has_pre:
        UST = [sb("UST%d" % i, [128, 1024], BF16, at=GHo[i][1])[0] for i in range(2)]
        USTb = GHb
    if has_post:
        WC = [sb("WC%d" % i, [128, 8, 128], BF16)[0] for i in range(2)]
        WCb = [Buf(), Buf()]
        WD = [sb("WD%d" % i, [128, 4, 128], BF16)[0] for i in range(2)]
        WDb = [Buf(), Buf()]
        UP = [sb("UP%d" % i, [128, 2, 1024], BF16)[0] for i in range(2)]
        UPb = [Buf(), Buf()]
        TP = [sb("TP%d" % i, [128, 2, 2, 512], BF16)[0] for i in range(2)]
        TPb = [Buf(), Buf()]
        FT, _ = sb("FT", [128, 4, TT], BF16); FTb = [Buf() for _ in range(4)]
        QT, _ = sb("QT", [128, 4, TT], BF16); QTb = [Buf() for _ in range(4)]
        OT, _ = sb("OT", [128, 4, TT], BF16); OTb = [Buf() for _ in range(4)]
        KT, _ = sb("KT", [128, 4, 256], BF16); KTb = Buf()
        VV, _ = sb("VV", [128, 2, 512], BF16); VVb = Buf()
        MEMB, _ = sb("MEMB", [128, NDC, 256], BF16, at=act_off + 32 * 1024)
        PP = [sb("PP%d" % i, [128, 256], F32)[0] for i in range(2)]
        PPb = [Buf(), Buf()]
        PTB = [sb("PTB%d" % i, [128, 2, 128], BF16)[0] for i in range(2)]
        PTBb = [Buf(), Buf()]
        SM = [sb("SM%d" % i, [128, 4], F32)[0] for i in range(2)]
        SMb = [Buf(), Buf()]
        GT = TMP
        GTb = TMPb

    PS = [nc.alloc_psum_tensor("ps%d" % i, [128, 512], F32) for i in range(8)]
    PSb = [Buf() for _ in range(8)]
    psi = [0]
    reserved = set()

    def bank(reserve=False):
        while True:
            i = psi[0] % 8
            psi[0] += 1
            if i not in reserved:
                break
        if reserve:
            reserved.add(i)
        return PS[i], PSb[i]

    def unreserve(pb):
        reserved.discard(PSb.index(pb))

    class Stream:
        def __init__(self, dram, slots, slot_bufs, per_slot, view):
            self.dram, self.slots, self.bufs, self.per = dram, slots, slot_bufs, per_slot
            self.view = view
            self.order = []
            self.pos = 0
            self.loaded = 0
            self.nuse = 0

        def plan(self, first_tile):
            self.order.append(first_tile)

        def prefetch(self):
            ns = len(self.slots)
            while self.loaded < len(self.order) and self.loaded < self.pos + ns:
                g = self.order[self.loaded]
                s = self.loaded % ns
                src = self.dram.ap()[g:g + self.per].rearrange("t p f -> p t f")
                C.dma(pool, self.view(self.slots[s]), src, reads=[], writes=[self.bufs[s]])
                self.loaded += 1

        def get(self, first_tile):
            assert self.order[self.pos] == first_tile, (self.order[self.pos], first_tile)
            self.prefetch()
            s = self.pos % len(self.slots)
            self.pos += 1
            return self.slots[s], self.bufs[s]

        def release(self):
            self.prefetch()

    S2048 = Stream(w2048, WA, WAb, 2, lambda t: t[:].rearrange("p t k m -> p t (k m)"))
    S5632 = Stream(w5632, WB, WBb, 1, lambda t: t[:].rearrange("p j m -> p (j m)").rearrange("p (t f) -> p t f", t=1))
    if has_post:
        S1024 = Stream(w1024, WC, WCb, 1, lambda t: t[:].rearrange("p j m -> p (j m)").rearrange("p (t f) -> p t f", t=1))
        S512 = Stream(w512, WD, WDb, 1, lambda t: t[:].rearrange("p j m -> p (j m)").rearrange("p (t f) -> p t f", t=1))

    o = 0
    if has_post:
        T_Q = o; o += 4
        T_KV = o; o += 8
        T_GATE = o; o += 48
        T_GATE_PAD = o; o += 16
        T_WOUT = o; o += 16
        T_F2 = o; o += 88
    if has_pre:
        T_F1 = o; o += 88
        T_GLU = o; o += 16
        T_FOU = o; o += 4
    assert o == n2048, (o, n2048)
    o5 = 0
    if has_post:
        T5_F2 = o5; o5 += 16
    if has_pre:
        T5_F1 = o5; o5 += 16
    assert o5 == n5632

    def vcol(layer_slot, col):
        return VEC[:, layer_slot * V_N + col: layer_slot * V_N + col + 1]

    def layer_norm(z_ap, z_bufs, nch, g_col, b_col, vslot, eps, out_f32=None, out_bf=None,
                   out_bufs_f32=None, out_bufs_bf=None, silu=False):
        nfeat = nch * 128
        C.op(dve, lambda h: h.memset(ACC1[:], 0.0), writes=[ACC1b])
        C.op(dve, lambda h: h.memset(ACC2[:], 0.0), writes=[ACC2b])
        for c in range(nch):
            s = c % 2
            C.op(act, lambda h, c=c, s=s: h.activation(out=SQ[s][:], in_=z_ap(c), func=AF.Square),
                 reads=z_bufs(c), writes=[SQb[s]])
            C.op(dve, lambda h, c=c: h.tensor_tensor(out=ACC1[:], in0=ACC1[:], in1=z_ap(c), op=ALU.add),
                 reads=z_bufs(c) + [ACC1b], writes=[ACC1b])
            C.op(dve, lambda h, s=s: h.tensor_tensor(out=ACC2[:], in0=ACC2[:], in1=SQ[s][:], op=ALU.add),
                 reads=[SQb[s], ACC2b], writes=[ACC2b])
        p1, p1b = bank()
        C.mm_group(p1[:], p1b, [(ONES[:], ACC1[:], [CONb, ACC1b])])
        p2, p2b = bank()
        C.mm_group(p2[:], p2b, [(ONES[:], ACC2[:], [CONb, ACC2b])])
        C.op(dve, lambda h: h.tensor_scalar(out=MEAN[:], in0=p1[:], scalar1=1.0 / nfeat, scalar2=None, op0=ALU.mult),
             reads=[p1b], writes=[MEANb])
        C.op(dve, lambda h: h.tensor_tensor(out=TMP[0][:], in0=MEAN[:], in1=MEAN[:], op=ALU.mult),
             reads=[MEANb], writes=[TMPb[0]])
        C.op(dve, lambda h: h.scalar_tensor_tensor(out=RSTD[:], in0=p2[:], scalar=1.0 / nfeat, in1=TMP[0][:],
                                                   op0=ALU.mult, op1=ALU.subtract),
             reads=[p2b, TMPb[0]], writes=[RSTDb])
        C.op(dve, lambda h: h.tensor_scalar(out=RSTD[:], in0=RSTD[:], scalar1=float(eps), scalar2=None, op0=ALU.add),
             reads=[RSTDb], writes=[RSTDb])
        C.op(act, lambda h: h.activation(out=RSTD[:], in_=RSTD[:], func=AF.Sqrt),
             reads=[RSTDb], writes=[RSTDb])
        C.op(dve, lambda h: h.reciprocal(out=RSTD[:], in_=RSTD[:]),
             reads=[RSTDb], writes=[RSTDb])
        for c in range(nch):
            ne = pool if c % 3 == 2 else dve
            C.op(ne, lambda h, c=c: h.tensor_tensor(out=z_ap(c), in0=z_ap(c), in1=MEAN[:], op=ALU.subtract),
                 reads=z_bufs(c) + [MEANb], writes=z_bufs(c))
            C.op(ne, lambda h, c=c: h.tensor_tensor(out=z_ap(c), in0=z_ap(c), in1=RSTD[:], op=ALU.mult),
                 reads=z_bufs(c) + [RSTDb], writes=z_bufs(c))
            g = vcol(vslot, g_col + c)
            b = vcol(vslot, b_col + c)
            if silu:
                C.op(act, lambda h, c=c, g=g, b=b: h.activation(out=out_bf(c), in_=z_ap(c), func=AF.Silu, bias=b, scale=g),
                     reads=z_bufs(c) + [VECb], writes=out_bufs_bf(c))
            else:
                C.op(act, lambda h, c=c, g=g, b=b: h.activation(out=out_f32(c), in_=z_ap(c), func=AF.Identity, bias=b, scale=g),
                     reads=z_bufs(c) + [VECb], writes=out_bufs_f32(c))
                C.op(act, lambda h, c=c: h.activation(out=out_bf(c), in_=out_f32(c), func=AF.Copy),
                     reads=out_bufs_f32(c), writes=out_bufs_bf(c))

    def x_ap(c):
        return X[:, c, :]

    def x_bufs(c):
        return [Xb[c]]

    def xb_ap(c):
        return XB[:, c, :]

    def xb_bufs(c):
        return [XBb[c]]

    def ln_x(g_col, b_col, vslot):
        layer_norm(x_ap, x_bufs, NDC, g_col, b_col, vslot, LN_EPS / ALPHA ** 2,
                   out_f32=x_ap, out_bf=xb_ap, out_bufs_f32=x_bufs, out_bufs_bf=xb_bufs)

    def plan_ffn(t13, t2):
        for j in range(NJ):
            S2048.plan(t13 + 2 * j)
        for dc in range(NDC):
            S5632.plan(t2 + dc)

    def ffn(t13, t2, g_col, b_col, vslot):
        for j in range(NJ):
            w, wb = S2048.get(t13 + 2 * j)
            pa, pab = bank()
            pg, pgb = bank()
            C.mm_group(pa[:], pab, [(w[:, 0, kc, :], XB[:, kc, :], [wb, XBb[kc]]) for kc in range(NDC)])
            C.mm_group(pg[:], pgb, [(w[:, 1, kc, :], XB[:, kc, :], [wb, XBb[kc]]) for kc in range(NDC)])
            S2048.release()
            s = j % 2
            C.op(act, lambda h, s=s, pg=pg: h.activation(out=TMP[1 + s][:], in_=pg[:], func=AF.Silu),
                 reads=[pgb], writes=[TMPb[1 + s]])
            C.op(dve, lambda h, s=s, pa=pa, j=j: h.tensor_tensor(out=ACTR[:, j, :], in0=TMP[1 + s][:], in1=pa[:], op=ALU.mult),
                 reads=[TMPb[1 + s], pab], writes=[ACTb[j]])
        for dc in range(NDC):
            w, wb = S5632.get(t2 + dc)
            py, pyb = bank()
            C.mm_group(py[:], pyb, [(w[:, j, :], ACTR[:, j, :], [wb, ACTb[j]]) for j in range(NJ)])
            S5632.release()
            C.op(dve, lambda h, dc=dc, py=py: h.scalar_tensor_tensor(out=X[:, dc, :], in0=py[:], scalar=0.5 / ALPHA, in1=X[:, dc, :],
                                                                     op0=ALU.mult, op1=ALU.add),
                 reads=[pyb, Xb[dc]], writes=[Xb[dc]])
        ln_x(g_col, b_col, vslot)

    def plan_pre():
        for cc in range(8):
            S2048.plan(T_GLU + 2 * cc)
        for gg in range(2):
            S2048.plan(T_FOU + 2 * gg)

    def pre_mixer(t0, vslot):
        C.dma(sp, x1o.ap().rearrange("(c p) t -> p c t", p=128)[:, :, t0:t0 + TT], X[:],
              reads=Xb, writes=[x1o_b])
        for cc in range(8):
            w, wb = S2048.get(T_GLU + 2 * cc)
            pv, pvb = bank()
            pg, pgb = bank()
            C.mm_group(pv[:], pvb, [(w[:, 0, kc, :], XB[:, kc, :], [wb, XBb[kc]]) for kc in range(NDC)])
            C.mm_group(pg[:], pgb, [(w[:, 1, kc, :], XB[:, kc, :], [wb, XBb[kc]]) for kc in range(NDC)])
            S2048.release()
            s = cc % 2
            bv = vcol(vslot, V_BIN + cc)
            bg = vcol(vslot, V_BIN + 8 + cc)
            C.op(act, lambda h, s=s, pg=pg, bg=bg: h.activation(out=TMP[1 + s][:], in_=pg[:], func=AF.Sigmoid, bias=bg),
                 reads=[pgb, VECb], writes=[TMPb[1 + s]])
            C.op(dve, lambda h, s=s, pv=pv, bv=bv, cc=cc: h.scalar_tensor_tensor(out=CONV[:, cc, :], in0=pv[:], scalar=bv, in1=TMP[1 + s][:],
                                                                               op0=ALU.add, op1=ALU.mult),
                 reads=[pvb, VECb, TMPb[1 + s]], writes=conv_bufs(cc))
        C.dma(sp, gluo.ap().rearrange("(c p) t -> p c t", p=128)[:, :, t0:t0 + TT], CONV[:],
              reads=[ACTb[i] for i in range(16)], writes=[gluo_b])
        for gg in range(2):
            w, wb = S2048.get(T_FOU + 2 * gg)
            for i in range(2):
                g = 2 * gg + i
                pu, pub = bank()
                C.mm_group(pu[:], pub, [(w[:, i, kc, :], XB[:, kc, :], [wb, XBb[kc]]) for kc in range(NDC)])
                bf = vcol(vslot, V_BIN + 16 + g)
                C.op(act, lambda h, g=g, pu=pu, bf=bf: h.activation(out=HCB[:, g, :], in_=pu[:], func=AF.Identity, bias=bf),
                     reads=[pub, VECb], writes=[ACTb[16 + g]])
            S2048.release()
        for tb in range(TT // 128):
            s = tb % 2
            for half in range(2):
                pU, pUb = bank()
                for i in range(2):
                    g = 2 * half + i
                    C.mm_group(pU[:, i * 256:(i + 1) * 256], pUb,
                               [(HCB[:, g, tb * 128:(tb + 1) * 128], DFTC[:], [ACTb[16 + g], DFTCb])])
                C.op(act, lambda h, s=s, half=half, pU=pU: h.activation(out=UST[s][:, half * 512:(half + 1) * 512], in_=pU[:], func=AF.Copy),
                     reads=[pUb], writes=[USTb[s]])
            C.dma(sp, uo.ap()[t0 + tb * 128: t0 + (tb + 1) * 128, :], UST[s][:], reads=[USTb[s]], writes=[uo_b])

    def plan_post():
        for hh in range(2):
            S2048.plan(T_Q + 2 * hh)
        for dc in range(NDC):
            S2048.plan(T_GATE + 4 * dc)
            S2048.plan(T_GATE + 4 * dc + 2)
            S1024.plan(dc)
            S512.plan(dc)
            S512.plan(NDC + dc)
        for dc in range(0, NDC, 2):
            S2048.plan(T_WOUT + dc)

    def kv_setup():
        C.dma(pool, MEMB[:], memT.ap().rearrange("(c p) m -> p c m", p=128), reads=[], writes=ACTb[32:40])
        for hh in range(2):
            w, wb = S2048.get(T_KV + 2 * hh)
            for i in range(2):
                hd = 2 * hh + i
                pk, pkb = bank()
                C.mm_group(pk[:, 0:256], pkb, [(w[:, i, kc, :], MEMB[:, kc, :], [wb] + ACTb[32:40]) for kc in range(NDC)])
                C.op(act, lambda h, hd=hd, pk=pk: h.activation(out=KT[:, hd, :], in_=pk[:, 0:256], func=AF.Copy),
                     reads=[pkb], writes=[KTb])
            S2048.release()
        pv = [bank(), bank()]
        for vv in range(2):
            w, wb = S2048.get(T_KV + 4 + 2 * vv)
            for i in range(2):
                vt = 2 * vv + i
                for mc in range(2):
                    C.mm_group(pv[mc][0][:, vt * 128:(vt + 1) * 128], pv[mc][1],
                               [(MEMB[:, kc, mc * 128:(mc + 1) * 128], w[:, i, kc, :], [wb] + ACTb[32:40]) for kc in range(NDC)])
            S2048.release()
        for mc in range(2):
            C.op(act, lambda h, mc=mc: h.activation(out=VV[:, mc, :], in_=pv[mc][0][:], func=AF.Copy),
                 reads=[pv[mc][1]], writes=[VVb])

    def plan_kv():
        for i in range(4):
            S2048.plan(T_KV + 2 * i)

    def mixer(t, vslot):
        t0 = t * TT
        for cc in range(8):
            s = cc % 2
            C.dma(sp, GH[s][:], gluh.ap()[cc * 128:(cc + 1) * 128, t0:t0 + TT + 2 * HALO], reads=[], writes=[GHb[s]])
            eng = dve
            wcol = lambda tap, cc=cc: vcol(vslot, V_CONVW + cc * 31 + tap)
            cb = vcol(vslot, V_CONVB + cc)
            C.op(eng, lambda h, s=s, cc=cc, cb=cb, wcol=wcol: h.tensor_scalar(out=CONV[:, cc, :], in0=GH[s][:, 0:TT], scalar1=wcol(0), scalar2=cb,
                                                                             op0=ALU.mult, op1=ALU.add),
                 reads=[GHb[s], VECb], writes=conv_bufs(cc))
            for tap in range(1, 31):
                C.op(eng, lambda h, s=s, cc=cc, tap=tap, wcol=wcol: h.scalar_tensor_tensor(out=CONV[:, cc, :], in0=GH[s][:, tap:tap + TT], scalar=wcol(tap),
                                                                                          in1=CONV[:, cc, :], op0=ALU.mult, op1=ALU.add),
                     reads=[GHb[s], VECb] + conv_bufs(cc), writes=conv_bufs(cc))
        layer_norm(lambda c: CONV[:, c, :], conv_bufs, 8, V_CLNG, V_CLNB, vslot, LN_EPS,
                   out_bf=lambda c: HCB[:, c, :], out_bufs_bf=lambda c: [ACTb[16 + c]], silu=True)
        pf = [bank() for _ in range(4)]
        for pc in range(32):
            s = pc % 2
            C.dma(sp, UP[s][:], ufull.ap()[pc * 256:(pc + 1) * 256, :].rearrange("(a p) c -> p a c", p=128), reads=[], writes=[UPb[s]])
            C.dma(sp, TP[s][:].rearrange("p a c k -> p (a c k)"), ftab.ap()[t, pc], reads=[], writes=[TPb[s]])
            for a in range(2):
                for g in range(4):
                    first_ = (pc == 0 and a == 0)
                    last_ = (pc == 31 and a == 1)
                    C.mm_group(pf[g][0][:], pf[g][1],
                               [(UP[s][:, a, g * 256:g * 256 + 128], TP[s][:, a, 0, :], [UPb[s], TPb[s]]),
                                (UP[s][:, a, g * 256 + 128:g * 256 + 256], TP[s][:, a, 1, :], [UPb[s], TPb[s]])],
                               signal=(g == 3 and a == 1), start=first_, stop=last_)
        for g in range(4):
            C.op(act, lambda h, g=g: h.activation(out=FT[:, g, :], in_=pf[g][0][:], func=AF.Copy),
                 reads=[pf[g][1]], writes=[FTb[g]])
        for hh in range(2):
            w, wb = S2048.get(T_Q + 2 * hh)
            for i in range(2):
                hd = 2 * hh + i
                pq, pqb = bank()
                C.mm_group(pq[:], pqb, [(w[:, i, kc, :], XB[:, kc, :], [wb, XBb[kc]]) for kc in range(NDC)])
                bq = vcol(vslot, V_BIN + 20 + hd)
                C.op(act, lambda h, hd=hd, pq=pq, bq=bq: h.activation(out=QT[:, hd, :], in_=pq[:], func=AF.Identity, bias=bq),
                     reads=[pqb, VECb], writes=[QTb[hd]])
            S2048.release()
        it = 0
        for hd in range(4):
            po, pob = bank(reserve=True)
            for tb in range(4):
                s = it % 2
                it += 1
                ps_, psb = bank()
                C.mm_group(ps_[:, 0:256], psb, [(QT[:, hd, tb * 128:(tb + 1) * 128], KT[:, hd, :], [QTb[hd], KTb])])
                C.op(dve, lambda h, s=s, ps_=ps_: h.reduce_max(out=SM[s][:, 0:1], in_=ps_[:, 0:256], axis=AX.X),
                     reads=[psb], writes=[SMb[s]])
                C.op(dve, lambda h, s=s: h.tensor_scalar(out=SM[s][:, 1:2], in0=SM[s][:, 0:1], scalar1=-SCALE, scalar2=None, op0=ALU.mult),
                     reads=[SMb[s]], writes=[SMb[s]])
                C.op(act, lambda h, s=s, ps_=ps_: h.activation(out=PP[s][:], in_=ps_[:, 0:256], func=AF.Exp, bias=SM[s][:, 1:2], scale=SCALE),
                     reads=[psb, SMb[s]], writes=[PPb[s]])
                C.op(dve, lambda h, s=s: h.reduce_sum(out=SM[s][:, 2:3], in_=PP[s][:], axis=AX.X),
                     reads=[PPb[s]], writes=[SMb[s]])
                C.op(dve, lambda h, s=s: h.reciprocal(out=SM[s][:, 3:4], in_=SM[s][:, 2:3]),
                     reads=[SMb[s]], writes=[SMb[s]])
                C.op(dve, lambda h, s=s: h.tensor_scalar(out=PP[s][:], in0=PP[s][:], scalar1=SM[s][:, 3:4], scalar2=None, op0=ALU.mult),
                     reads=[PPb[s], SMb[s]], writes=[PPb[s]])
                for mc in range(2):
                    pt, ptb = bank()
                    C.mm_group(pt[:, 0:128], ptb, [(PP[s][:, mc * 128:(mc + 1) * 128], IDENT[:], [PPb[s], CONb])])
                    C.op(act, lambda h, s=s, mc=mc, pt=pt: h.activation(out=PTB[s][:, mc, :], in_=pt[:, 0:128], func=AF.Copy),
                         reads=[ptb], writes=[PTBb[s]])
                C.mm_group(po[:, tb * 128:(tb + 1) * 128], pob,
                           [(VV[:, mc, hd * 128:(hd + 1) * 128], PTB[s][:, mc, :], [VVb, PTBb[s]]) for mc in range(2)])
            C.op(act, lambda h, hd=hd, po=po: h.activation(out=OT[:, hd, :], in_=po[:], func=AF.Copy),
                 reads=[pob], writes=[OTb[hd]])
            unreserve(pob)
        for dc in range(NDC):
            w01, w01b = S2048.get(T_GATE + 4 * dc)
            pg = [bank(), bank(), bank()]
            for i in range(2):
                C.mm_group(pg[i][0][:], pg[i][1], [(w01[:, i, kc, :], XB[:, kc, :], [w01b, XBb[kc]]) for kc in range(NDC)])
            S2048.release()
            w2_, w2b_ = S2048.get(T_GATE + 4 * dc + 2)
            C.mm_group(pg[2][0][:], pg[2][1], [(w2_[:, 0, kc, :], XB[:, kc, :], [w2b_, XBb[kc]]) for kc in range(NDC)])
            S2048.release()
            wc, wcb = S1024.get(dc)
            pyc, pycb = bank()
            C.mm_group(pyc[:], pycb, [(wc[:, cc, :], HCB[:, cc, :], [wcb, ACTb[16 + cc]]) for cc in range(8)])
            S1024.release()
            wf, wfb = S512.get(dc)
            pyf, pyfb = bank()
            C.mm_group(pyf[:], pyfb, [(wf[:, g, :], FT[:, g, :], [wfb, FTb[g]]) for g in range(4)])
            S512.release()
            wm, wmb = S512.get(NDC + dc)
            pym, pymb = bank()
            C.mm_group(pym[:], pymb, [(wm[:, g, :], OT[:, g, :], [wmb, OTb[g]]) for g in range(4)])
            S512.release()
            ys = [(pyc, pycb), (pyf, pyfb), (pym, pymb)]
            for b in range(3):
                bgate = vcol(vslot, V_BIN + 24 + b * 16 + dc)
                C.op(act, lambda h, b=b, bgate=bgate: h.activation(out=GT[b][:], in_=pg[b][0][:], func=AF.Sigmoid, bias=bgate),
                     reads=[pg[b][1], VECb], writes=[GTb[b]])
                C.op(dve, lambda h, b=b: h.tensor_tensor(out=GT[b][:], in0=GT[b][:], in1=ys[b][0][:], op=ALU.mult),
                     reads=[GTb[b], ys[b][1]], writes=[GTb[b]])
            C.op(dve, lambda h: h.tensor_tensor(out=GT[0][:], in0=GT[0][:], in1=GT[1][:], op=ALU.add),
                 reads=[GTb[0], GTb[1]], writes=[GTb[0]])
            C.op(dve, lambda h, dc=dc: h.tensor_tensor(out=MRG[:, dc, :], in0=GT[0][:], in1=GT[2][:], op=ALU.add),
                 reads=[GTb[0], GTb[2]], writes=[ACTb[24 + dc]])
        for dp in range(0, NDC, 2):
            w, wb = S2048.get(T_WOUT + dp)
            for i in range(2):
                dc = dp + i
                py, pyb = bank()
                C.mm_group(py[:], pyb, [(w[:, i, kc, :], MRG[:, kc, :], [wb, ACTb[24 + kc]]) for kc in range(NDC)])
                C.op(dve, lambda h, dc=dc, py=py: h.scalar_tensor_tensor(out=X[:, dc, :], in0=py[:], scalar=1.0 / ALPHA, in1=X[:, dc, :],
                                                                         op0=ALU.mult, op1=ALU.add),
                     reads=[pyb, Xb[dc]], writes=[Xb[dc]])
            S2048.release()
        ln_x(V_LN2G, V_LN2B, vslot)

    if has_post:
        plan_kv()
    for t in range(NT):
        if has_post:
            plan_post()
            plan_ffn(T_F2, T5_F2)
        if has_pre:
            plan_ffn(T_F1, T5_F1)
            plan_pre()

    C.dma(sp, VEC[:], vecs.ap(), reads=[], writes=[VECb])
    C.dma(sp, IDENT[:], consts.ap()[:, 0:128], reads=[], writes=[CONb])
    C.dma(sp, ONES[:], consts.ap()[:, 128:256], reads=[], writes=[CONb])
    C.dma(pool, DFTC[:], consts.ap()[:, 256:512], reads=[], writes=[DFTCb])
    if has_post:
        kv_setup()
    post_slot = 0
    pre_slot = 1 if has_post else 0
    for t in range(NT):
        t0 = t * TT
        C.dma(sp, X[:], xin.ap().rearrange("(c p) t -> p c t", p=128)[:, :, t0:t0 + TT], reads=[], writes=Xb)
        for c in range(NDC):
            C.op(act, lambda h, c=c: h.activation(out=XB[:, c, :], in_=X[:, c, :], func=AF.Copy),
                 reads=[Xb[c]], writes=[XBb[c]])
        if has_post:
            mixer(t, post_slot)
            ffn(T_F2, T5_F2, V_LN3G, V_LN3B, post_slot)
        if has_pre:
            ffn(T_F1, T5_F1, V_LN1G, V_LN1B, pre_slot)
            pre_mixer(t0, pre_slot)
        if last:
            C.dma(sp, yo.ap().rearrange("(c p) t -> p c t", p=128)[:, :, t0:t0 + TT], X[:], reads=Xb, writes=[yo_b])
    C.finish()
    return nc


def _tile_k(W, ncols=128):
    K, N = W.shape
    return np.ascontiguousarray(W.reshape(K // 128, 128, N // 128, 128).transpose(2, 1, 0, 3)).reshape(N // 128, 128, K)


def _vec_cols(v):
    return np.ascontiguousarray(v.reshape(-1, 128).T)


def _layer_vecs(I, l):
    cols = [I["ln1_g"][l], I["ln1_b"][l], I["ln2_g"][l], I["ln2_b"][l], I["ln3_g"][l], I["ln3_b"][l],
            I["b_in"][l], I["conv_b"][l], I["conv_ln_g"][l], I["conv_ln_b"][l]]
    out = [_vec_cols(c) for c in cols]
    cw = I["conv_w"][l]
    out.append(np.ascontiguousarray(cw.T.reshape(8, 128, 31).transpose(1, 0, 2)).reshape(128, 248))
    v = np.concatenate(out, axis=1)
    assert v.shape == (128, V_N), v.shape
    return v.astype(np.float32)


def _w13_tiles(w13):
    t = _tile_k(w13)
    out = np.empty_like(t)
    out[0::2] = t[:NJ]
    out[1::2] = t[NJ:]
    return out


def _consts():
    c = np.zeros((128, 512), np.float32)
    c[:, 0:128] = np.eye(128, dtype=np.float32)
    c[:, 128:256] = 1.0
    k = np.arange(128)
    ang = 2.0 * np.pi * np.outer(k, k) / 128.0
    c[:, 256:384] = np.cos(ang) / 32.0
    c[:, 384:512] = np.sin(ang) / 32.0
    return c


_FTAB = {}


def _ftab(j):
    if j in _FTAB:
        return _FTAB[j]
    s = np.arange(SEQ, dtype=np.int64)
    k = np.arange(TOK, dtype=np.int64) + TOK * j
    ph = (np.outer(s, k) % SEQ).astype(np.float64) * (2.0 * np.pi / SEQ)
    tab = np.stack([np.cos(ph) / 32.0, -np.sin(ph) / 32.0], axis=1).astype(np.float32)
    tab = tab.reshape(32, 2, 128, 2, NT, TT)
    tab = np.ascontiguousarray(tab.transpose(4, 0, 2, 1, 3, 5)).reshape(NT, 32, 128, 2 * 2 * TT)
    tab = tab.astype(ml_dtypes.bfloat16)
    _FTAB[j] = tab
    return tab


def _post_weights(I, l):
    w_in = I["w_in"][l]
    tq = _tile_k(w_in[:, 2560:3072])
    tkv = _tile_k(I["w_mem_kv"][l])
    tg = _tile_k(w_in[:, 3072:])
    gates = np.zeros((64, 128, 2048), np.float32)
    for dc in range(NDC):
        gates[4 * dc + 0] = tg[0 * 16 + dc]
        gates[4 * dc + 1] = tg[1 * 16 + dc]
        gates[4 * dc + 2] = tg[2 * 16 + dc]
    two = [tq, tkv, gates, _tile_k(I["w_out"][l]), _w13_tiles(I["ffn2_w13"][l])]
    five = [_tile_k(I["ffn2_w2"][l])]
    one = _tile_k(I["w_conv_out"][l])
    half = np.concatenate([_tile_k(I["w_fourier"][l]), _tile_k(I["w_mem_o"][l])], axis=0)
    return two, five, one, half


def _pre_weights(I, l):
    w_in = I["w_in"][l]
    tglu = _tile_k(w_in[:, 0:2048])
    glu = np.empty_like(tglu)
    glu[0::2] = tglu[:8]
    glu[1::2] = tglu[8:]
    two = [_w13_tiles(I["ffn1_w13"][l]), glu, _tile_k(w_in[:, 2048:2560])]
    five = [_tile_k(I["ffn1_w2"][l])]
    return two, five


_PROGS = {}


def _get_prog(has_post, has_pre, first, last):
    key = (has_post, has_pre, first, last)
    if key not in _PROGS:
        n2048 = (4 + 8 + 64 + 16 + 88 if has_post else 0) + (88 + 16 + 4 if has_pre else 0)
        n5632 = (16 if has_post else 0) + (16 if has_pre else 0)
        _PROGS[key] = build_program(has_post, has_pre, first, last, n2048, n5632, 16, 32)
    return _PROGS[key]


def _run_segment(I, xT_cores, post_l, pre_l, extra, cores):
    has_post = post_l is not None
    has_pre = pre_l is not None
    nc = _get_prog(has_post, has_pre, post_l is None, pre_l is None)
    two, five = [], []
    vec = np.zeros((128, 2 * V_N), np.float32)
    slot = 0
    common = {}
    if has_post:
        t2, t5, one, half = _post_weights(I, post_l)
        two += t2
        five += t5
        common["w1024"] = one
        common["w512"] = half
        vec[:, slot * V_N:(slot + 1) * V_N] = _layer_vecs(I, post_l)
        slot += 1
    if has_pre:
        t2, t5 = _pre_weights(I, pre_l)
        two += t2
        five += t5
        vec[:, slot * V_N:(slot + 1) * V_N] = _layer_vecs(I, pre_l)
    common["w2048"] = np.concatenate(two, axis=0)
    common["w5632"] = np.concatenate(five, axis=0)
    common["vecs"] = vec
    common["consts"] = _consts()
    in_maps = []
    for c in cores:
        m = dict(common)
        m["xin"] = xT_cores[c]
        if has_post:
            m["gluh"] = extra["gluh"][c]
            m["ufull"] = extra["ufull"][c // 4]
            m["ftab"] = _ftab(c % 4)
            m["memT"] = extra["memT"][c // 4]
        in_maps.append(m)
    res = run_bass_kernel_spmd(nc, in_maps, core_ids=list(range(len(cores))))
    return res.results


def _exchange(results, cores=range(NCORES)):
    x1 = {c: results[i]["x1o"] for i, c in enumerate(cores)}
    glu = {c: results[i]["gluo"] for i, c in enumerate(cores)}
    uo = {c: results[i]["uo"] for i, c in enumerate(cores)}
    ufull = {}
    for b in range(2):
        if all((4 * b + j) in uo for j in range(4)):
            ufull[b] = np.concatenate([uo[4 * b + j] for j in range(4)], axis=0)
    gluh = {}
    for c in cores:
        g = np.zeros((1024, TOK + 2 * HALO), np.float32)
        g[:, HALO:HALO + TOK] = glu[c]
        if c % 4 != 0 and (c - 1) in glu:
            g[:, :HALO] = glu[c - 1][:, TOK - HALO:]
        if c % 4 != 3 and (c + 1) in glu:
            g[:, HALO + TOK:] = glu[c + 1][:, :HALO]
        gluh[c] = g
    return x1, {"gluh": gluh, "ufull": ufull}


def kernel(**inputs):
    I = {k: np.asarray(v) for k, v in inputs.items()}
    x = I["x"]
    cores = list(range(NCORES))
    xT = {c: np.ascontiguousarray(x[c // 4, (c % 4) * TOK:(c % 4 + 1) * TOK, :].T) for c in cores}
    memT = {b: np.ascontiguousarray(I["mem"][b].T) for b in range(2)}
    rA = _run_segment(I, xT, None, 0, None, cores)
    x1, ex = _exchange(rA)
    ex["memT"] = memT
    rB = _run_segment(I, x1, 0, 1, ex, cores)
    x1, ex = _exchange(rB)
    ex["memT"] = memT
    rC = _run_segment(I, x1, 1, None, ex, cores)
    out = np.empty((2, SEQ, D), np.float32)
    for i, c in enumerate(cores):
        out[c // 4, (c % 4) * TOK:(c % 4 + 1) * TOK, :] = rC[i]["yo"].T
    return out
```
